# Optimizing a Trainium2 kernel written in Bass

```python
import jax, jax.numpy as jnp
from jax import lax
import numpy as np

D_MODEL = 1024
BATCH = 16
SEQ = 2048
DEPTH = 2

N_MIXERS = 2
GRID_W = 64
RMS_EPS = 1e-6
GN_EPS = 1e-6
ROPE_THETA = 10000.0

RET_HEADS = 4
RET_QK_DIM = D_MODEL // RET_HEADS
RET_V_DIM = 2 * RET_QK_DIM
RET_CHUNK = 128
RET_QK_W = RET_HEADS * RET_QK_DIM
RET_V_W = RET_HEADS * RET_V_DIM
RET_IN_W = 2 * RET_QK_W + 2 * RET_V_W

ATTN_HEAD_DIM = 128
ATTN_Q_HEADS = D_MODEL // ATTN_HEAD_DIM
ATTN_KV_HEADS = 2
ATTN_GROUP = ATTN_Q_HEADS // ATTN_KV_HEADS
ATTN_IN_W = (ATTN_Q_HEADS + 2 * ATTN_KV_HEADS) * ATTN_HEAD_DIM
Q_BLOCK = 128

D_FF = 4 * D_MODEL

N_RET_LAYERS = (DEPTH + 1) // 2
N_ATTN_LAYERS = DEPTH // 2

kernel_name = "hybrid_retention_gqa_axial_encoder"


def rms_norm(x, g):
    xf = x.astype(jnp.float32)
    y = xf * lax.rsqrt(jnp.mean(xf * xf, axis=-1, keepdims=True) + RMS_EPS)
    return (y * g.astype(jnp.float32)).astype(x.dtype)


def axial_rope_tables(seq, rot_dim):
    rows = seq // GRID_W
    row = jnp.broadcast_to(jnp.arange(rows, dtype=jnp.float32)[:, None], (rows, GRID_W)).reshape(seq)
    col = jnp.broadcast_to(jnp.arange(GRID_W, dtype=jnp.float32)[None, :], (rows, GRID_W)).reshape(seq)
    per_axis = rot_dim // 2
    n_freq = per_axis // 2
    inv_freq = ROPE_THETA ** (-jnp.arange(n_freq, dtype=jnp.float32) * 2.0 / per_axis)
    ang = jnp.concatenate([row[:, None] * inv_freq[None, :], col[:, None] * inv_freq[None, :]], axis=-1)
    return jnp.cos(ang), jnp.sin(ang)


def apply_rope(x, cos, sin):
    xf = x.astype(jnp.float32)
    half = xf.shape[-1] // 2
    x1, x2 = xf[..., :half], xf[..., half:]
    out = jnp.concatenate([x1 * cos - x2 * sin, x1 * sin + x2 * cos], axis=-1)
    return out.astype(x.dtype)


def chunkwise_retention(q, k, v, log_gamma, include_diag):
    B, H, S, dk = q.shape
    dv = v.shape[-1]
    C = RET_CHUNK
    N = S // C
    idx = jnp.arange(C, dtype=jnp.float32)
    lg = log_gamma.astype(jnp.float32)
    diff = idx[:, None] - idx[None, :]
    mask = (diff >= 0) if include_diag else (diff > 0)
    decay_intra = jnp.where(mask[None], jnp.exp(lg[:, None, None] * jnp.where(mask, diff, 0.0)[None]), 0.0)
    xi = jnp.exp(lg[:, None] * (idx[None, :] + 1.0))
    zeta = jnp.exp(lg[:, None] * (C - 1.0 - idx[None, :]))
    gamma_c = jnp.exp(lg * C)

    def to_chunks(t):
        return t.reshape(B, H, N, C, t.shape[-1]).transpose(2, 0, 1, 3, 4)

    def step(state, xs):
        qb, kb, vb = xs
        scores = jnp.einsum('bhid,bhjd->bhij', qb, kb) * decay_intra[None]
        intra = jnp.einsum('bhij,bhjv->bhiv', scores, vb)
        inter = jnp.einsum('bhid,bhdv->bhiv', qb, state) * xi[None, :, :, None]
        new_state = state * gamma_c[None, :, None, None] + jnp.einsum(
            'bhjd,bhjv->bhdv', kb * zeta[None, :, :, None], vb)
        return new_state, intra + inter

    state0 = jnp.zeros((B, H, dk, dv), jnp.float32)
    _, ys = lax.scan(step, state0, (to_chunks(q), to_chunks(k), to_chunks(v)))
    return ys.transpose(1, 2, 0, 3, 4).reshape(B, H, S, dv)


def retention_mixer(h, w_in, w_out, decay_fwd, decay_bwd, cos, sin):
    B, S, _ = h.shape
    proj = h @ w_in
    q, k, v, g = jnp.split(proj, [RET_QK_W, 2 * RET_QK_W, 2 * RET_QK_W + RET_V_W], axis=-1)
    q = q.reshape(B, S, RET_HEADS, RET_QK_DIM).transpose(0, 2, 1, 3)
    k = k.reshape(B, S, RET_HEADS, RET_QK_DIM).transpose(0, 2, 1, 3)
    v = v.reshape(B, S, RET_HEADS, RET_V_DIM).transpose(0, 2, 1, 3).astype(jnp.float32)
    q = apply_rope(q, cos, sin).astype(jnp.float32)
    k = apply_rope(k, cos, sin).astype(jnp.float32) * (RET_QK_DIM ** -0.5)
    log_gf = -jnp.exp(decay_fwd.astype(jnp.float32))
    log_gb = -jnp.exp(decay_bwd.astype(jnp.float32))
    y_fwd = chunkwise_retention(q, k, v, log_gf, include_diag=True)
    y_bwd = jnp.flip(chunkwise_retention(jnp.flip(q, 2), jnp.flip(k, 2), jnp.flip(v, 2),
                                         log_gb, include_diag=False), 2)
    y = y_fwd + y_bwd
    mu = jnp.mean(y, axis=-1, keepdims=True)
    var = jnp.mean(jnp.square(y - mu), axis=-1, keepdims=True)
    y = (y - mu) * lax.rsqrt(var + GN_EPS)
    y = y.transpose(0, 2, 1, 3).reshape(B, S, RET_V_W).astype(h.dtype)
    return (jax.nn.silu(g) * y) @ w_out


def attention_mixer(h, w_in, w_out, q_gain, k_gain, cos, sin):
    B, S, _ = h.shape
    proj = h @ w_in
    q_w = ATTN_Q_HEADS * ATTN_HEAD_DIM
    kv_w = ATTN_KV_HEADS * ATTN_HEAD_DIM
    q, k, v = jnp.split(proj, [q_w, q_w + kv_w], axis=-1)
    q = q.reshape(B, S, ATTN_KV_HEADS, ATTN_GROUP, ATTN_HEAD_DIM).transpose(0, 2, 3, 1, 4)
    k = k.reshape(B, S, ATTN_KV_HEADS, ATTN_HEAD_DIM).transpose(0, 2, 1, 3)
    v = v.reshape(B, S, ATTN_KV_HEADS, ATTN_HEAD_DIM).transpose(0, 2, 1, 3)
    q = apply_rope(rms_norm(q, q_gain), cos, sin) * (ATTN_HEAD_DIM ** -0.5)
    k = apply_rope(rms_norm(k, k_gain), cos, sin)
    nb = S // Q_BLOCK
    qb = q.reshape(B, ATTN_KV_HEADS, ATTN_GROUP, nb, Q_BLOCK, ATTN_HEAD_DIM).transpose(3, 0, 1, 2, 4, 5)

    def one_block(q_blk):
        s = jnp.einsum('bkgqd,bksd->bkgqs', q_blk, k).astype(jnp.float32)
        p = jax.nn.softmax(s, axis=-1).astype(v.dtype)
        return jnp.einsum('bkgqs,bksd->bkgqd', p, v)

    o = lax.map(one_block, qb)
    o = o.transpose(1, 0, 4, 2, 3, 5).reshape(B, S, D_MODEL)
    return o @ w_out


def sq_relu_mlp(h, w1, w2):
    return jnp.square(jax.nn.relu(h @ w1)) @ w2


def setup_inputs(seed: int = 0) -> dict:
    key = jax.random.key(seed)
    ks = jax.random.split(key, 16)
    f32 = jnp.float32

    def w(k, shape, fan_in):
        return jax.random.normal(k, shape, f32) * (fan_in ** -0.5)

    def gain(k, shape):
        return 1.0 + 0.05 * jax.random.normal(k, shape, f32)

    base = jnp.log(-jnp.log(1.0 - 2.0 ** (-5.0 - jnp.arange(RET_HEADS, dtype=f32))))
    return {
        "x": jax.random.normal(ks[0], (BATCH, SEQ, D_MODEL), f32),
        "norm_mix": gain(ks[1], (DEPTH, D_MODEL)),
        "norm_mlp": gain(ks[2], (DEPTH, D_MODEL)),
        "mlp_w1": w(ks[3], (DEPTH, D_MODEL, D_FF), D_MODEL),
        "mlp_w2": w(ks[4], (DEPTH, D_FF, D_MODEL), D_FF),
        "ret_w_in": w(ks[5], (N_RET_LAYERS, D_MODEL, RET_IN_W), D_MODEL),
        "ret_w_out": w(ks[6], (N_RET_LAYERS, RET_V_W, D_MODEL), RET_V_W),
        "ret_decay_fwd": base[None, :] + 0.05 * jax.random.normal(ks[7], (N_RET_LAYERS, RET_HEADS), f32),
        "ret_decay_bwd": base[None, :] + 0.05 * jax.random.normal(ks[8], (N_RET_LAYERS, RET_HEADS), f32),
        "attn_w_in": w(ks[9], (N_ATTN_LAYERS, D_MODEL, ATTN_IN_W), D_MODEL),
        "attn_w_out": w(ks[10], (N_ATTN_LAYERS, D_MODEL, D_MODEL), D_MODEL),
        "attn_q_norm": gain(ks[11], (N_ATTN_LAYERS, ATTN_HEAD_DIM)),
        "attn_k_norm": gain(ks[12], (N_ATTN_LAYERS, ATTN_HEAD_DIM)),
        "final_norm": gain(ks[13], (D_MODEL,)),
    }


def reference(x, norm_mix, norm_mlp, mlp_w1, mlp_w2, ret_w_in, ret_w_out, ret_decay_fwd,
              ret_decay_bwd, attn_w_in, attn_w_out, attn_q_norm, attn_k_norm, final_norm):
    S = x.shape[1]
    cos_r, sin_r = axial_rope_tables(S, RET_QK_DIM)
    cos_a, sin_a = axial_rope_tables(S, ATTN_HEAD_DIM)
    h = x
    for i in range(DEPTH):
        u = rms_norm(h, norm_mix[i])
        j = i // N_MIXERS
        if i % N_MIXERS == 0:
            h = h + retention_mixer(u, ret_w_in[j], ret_w_out[j], ret_decay_fwd[j],
                                    ret_decay_bwd[j], cos_r, sin_r)
        else:
            h = h + attention_mixer(u, attn_w_in[j], attn_w_out[j], attn_q_norm[j],
                                    attn_k_norm[j], cos_a, sin_a)
        u = rms_norm(h, norm_mlp[i])
        h = h + sq_relu_mlp(u, mlp_w1[i], mlp_w2[i])
    return rms_norm(h, final_norm)
```

```python
import contextlib
import numpy as np
import concourse.bass as bass
import concourse.mybir as mybir
from concourse.bass_utils import run_bass_kernel_spmd

F32 = mybir.dt.float32
BF16 = mybir.dt.bfloat16
AF = mybir.ActivationFunctionType
ALU = mybir.AluOpType
AX = mybir.AxisListType

NCORES = 8
D = 1024
SEQ = 2048
NSEQ = 2
C = 128
ENGS = ("pe", "act", "dve", "pool", "sp")


class Op:
    __slots__ = ("eng", "fn", "deps", "dma", "signal", "tick", "sem", "idx")

    def __init__(self, eng, fn, dma):
        self.eng = eng
        self.fn = fn
        self.deps = set()
        self.dma = dma
        self.signal = False
        self.tick = 0
        self.sem = None


class Sched:
    def __init__(self, nc):
        self.nc = nc
        self.ops = []
        self.state = {}
        self.final_waits = []
        self.last_on = {}
        self.last_dma = {}
        self.bar = ()

    def barrier(self):
        self.bar = tuple(set(self.last_on.values()) | set(self.last_dma.values()))

    def op(self, eng, fn, reads=(), writes=(), dma=False, dma_key=None):
        o = Op(eng, fn, dma)
        idx = len(self.ops)
        o.idx = idx
        o.deps.update(self.bar)
        for r in reads:
            st = self.state.get(r)
            if st is None:
                st = self.state[r] = [None, {}, []]
            if st[0] is not None:
                o.deps.add(st[0])
            if dma:
                st[2].append(idx)
            else:
                st[1][eng] = idx
        for w in writes:
            st = self.state.get(w)
            if st is None:
                st = self.state[w] = [None, {}, []]
            if st[0] is not None:
                o.deps.add(st[0])
            for r in st[1].values():
                if r != idx:
                    o.deps.add(r)
            for r in st[2]:
                if r != idx:
                    o.deps.add(r)
            st[0] = idx
            st[1] = {}
            st[2] = []
        if dma:
            o.sem = dma_key
            self.last_dma[dma_key] = idx
        self.ops.append(o)
        self.last_on[eng] = idx
        return idx

    def emit(self):
        nc = self.nc
        ops = self.ops
        per_eng = {e: [o for o in ops if o.eng == e] for e in ENGS}
        dma_counts = {}
        for e in ENGS:
            for i, o in enumerate(per_eng[e]):
                o.tick = i + 1
        for o in ops:
            if o.dma:
                dma_counts[o.sem] = dma_counts.get(o.sem, 0) + 16
                o.tick = dma_counts[o.sem]
        waits = {}
        for ename in ENGS:
            seen = {}
            for o in per_eng[ename]:
                need = {}
                for d in o.deps:
                    od = ops[d]
                    if od.dma:
                        key = ("d", od.sem)
                    else:
                        if od.eng == "pe" and ename == "pe":
                            continue
                        key = ("e", od.eng)
                    if key not in need or od.tick > need[key].tick:
                        need[key] = od
                wl = []
                for key, od in need.items():
                    if seen.get(key, 0) >= od.tick:
                        continue
                    seen[key] = od.tick
                    wl.append((key, od))
                    od.signal = True
                waits[o.idx] = wl
        real = {e: 0 for e in ENGS}
        for e in ENGS:
            for o in per_eng[e]:
                if o.dma:
                    o.signal = True
                elif o.signal:
                    real[e] += 1
                    o.tick = real[e]
        dma_keys = list(dma_counts.keys())
        with contextlib.ExitStack() as es:
            esem = {e: es.enter_context(nc.semaphore("s_" + e)) for e in ENGS}
            dsem = {k: es.enter_context(nc.semaphore("d_%d" % i)) for i, k in enumerate(dma_keys)}
            block = es.enter_context(nc.Block())

            def run_engine(ename, eng):
                for o in per_eng[ename]:
                    for key, od in waits[o.idx]:
                        s = dsem[key[1]] if key[0] == "d" else esem[key[1]]
                        eng.wait_ge(s, od.tick)
                    ins = o.fn(eng)
                    if o.signal:
                        if o.dma:
                            ins.then_inc(dsem[o.sem], 16)
                        else:
                            ins.then_inc(esem[ename], 1)
                if ename == "sp":
                    for k in self.final_waits:
                        eng.wait_ge(dsem[k], dma_counts[k])

            @block.tensor
            def _(e):
                run_engine("pe", e)

            @block.scalar
            def _(e):
                run_engine("act", e)

            @block.vector
            def _(e):
                run_engine("dve", e)

            @block.gpsimd
            def _(e):
                run_engine("pool", e)

            @block.sync
            def _(e):
                run_engine("sp", e)


class V:
    __slots__ = ("ap", "keys")

    def __init__(self, ap, keys):
        self.ap = ap
        self.keys = tuple(keys)


def _k(*vs):
    out = []
    for v in vs:
        if isinstance(v, V):
            out.extend(v.keys)
    return out


def _a(v):
    return v.ap if isinstance(v, V) else v


class Builder:
    def __init__(self, nc):
        self.nc = nc
        self.S = Sched(nc)

    def mm(self, out, lhsT, rhs, start, stop):
        self.S.op("pe", lambda e: e.matmul(out.ap, lhsT.ap, rhs.ap, start=start, stop=stop),
                  reads=_k(lhsT, rhs), writes=_k(out))

    def tr(self, out, in_, ident):
        self.S.op("pe", lambda e: e.transpose(out.ap, in_.ap, ident.ap), reads=_k(in_, ident), writes=_k(out))

    def act(self, out, in_, func, scale=1.0, bias=None, accum=None):
        def fn(e):
            kw = {}
            if bias is not None:
                kw["bias"] = _a(bias)
            if accum is not None:
                kw["accum_out"] = _a(accum)
            return e.activation(out.ap, in_.ap, func, scale=_a(scale), **kw)
        self.S.op("act", fn, reads=_k(in_, scale, bias), writes=_k(out, accum))

    def tt(self, eng, out, in0, in1, op):
        self.S.op(eng, lambda e: e.tensor_tensor(out.ap, in0.ap, in1.ap, op=op), reads=_k(in0, in1), writes=_k(out))

    def ts(self, eng, out, in0, s1, s2, op0, op1=None):
        def fn(e):
            if op1 is None:
                return e.tensor_scalar(out.ap, in0.ap, _a(s1), None, op0=op0)
            return e.tensor_scalar(out.ap, in0.ap, _a(s1), _a(s2), op0=op0, op1=op1)
        self.S.op(eng, fn, reads=_k(in0, s1, s2), writes=_k(out))

    def stt(self, eng, out, in0, scalar, in1, op0, op1):
        self.S.op(eng, lambda e: e.scalar_tensor_tensor(out.ap, in0.ap, _a(scalar), in1.ap, op0=op0, op1=op1),
                  reads=_k(in0, scalar, in1), writes=_k(out))

    def cp(self, eng, out, in_):
        if eng == "act":
            self.S.op("act", lambda e: e.copy(out.ap, in_.ap), reads=_k(in_), writes=_k(out))
        else:
            self.S.op(eng, lambda e: e.tensor_copy(out.ap, in_.ap), reads=_k(in_), writes=_k(out))

    def recip(self, out, in_):
        self.S.op("dve", lambda e: e.reciprocal(out.ap, in_.ap), reads=_k(in_), writes=_k(out))

    def memset(self, eng, out, val):
        self.S.op(eng, lambda e: e.memset(out.ap, val), writes=_k(out))

    def dma(self, eng, out, in_, key, reads=(), writes=()):
        self.S.op(eng, lambda e: e.dma_start(out=_a(out), in_=_a(in_)), reads=list(reads) + _k(in_),
                  writes=list(writes) + _k(out), dma=True, dma_key=key)


NCT = 7 * 128 + 2


def build(nseq=NSEQ, stages=("ret", "mlp0", "attn", "mlp1")):
    nc = bass.Bass("TRN2", target_bir_lowering=False)
    B = Builder(nc)
    S = B.S

    def din(name, shape):
        return nc.dram_tensor(name, list(shape), F32, kind="ExternalInput").ap()

    x_d = din("x", [nseq, SEQ, D])
    wr_in = din("ret_w_in", [D, 6144])
    wr_out = din("ret_w_out", [2048, D])
    w1_d = din("mlp_w1", [2, D, 4096])
    w2_d = din("mlp_w2", [2, 4096, D])
    wa_in = din("attn_w_in", [D, 1536])
    wa_out = din("attn_w_out", [D, D])
    ctab_d = din("ctab", [128, NCT])
    gains_d = din("gains", [128, 32])
    gfin_d = din("gfinal", [128, D])
    qkg_d = din("qkg", [128, 256])
    dec_d = din("dec", [128, 8])
    cosr_d = din("cos_r", [128, SEQ])
    sinr_d = din("sin_r", [128, SEQ])
    cosa_d = din("cos_a", [SEQ, 64])
    sina_d = din("sin_a", [SEQ, 64])
    y_d = nc.dram_tensor("y", [nseq, SEQ, D], F32, kind="ExternalOutput").ap()

    es = contextlib.ExitStack()
    with es:
        def sb(name, shape, dt):
            return es.enter_context(nc.sbuf_tensor("sb_" + name, list(shape), dt))

        xT_t = sb("xT", [128, 8, SEQ], F32)
        uT_t = sb("uT", [128, 8, SEQ], BF16)
        ring_t = sb("ring", [128, 4, 4096], BF16)
        ctab = sb("ctab", [128, NCT], F32)
        gains = sb("gains", [128, 32], F32)
        qkg = sb("qkg", [128, 256], F32)
        dec = sb("dec", [128, 8], F32)
        lg = sb("lg", [128, 8], F32)
        gc = sb("gc", [128, 8], F32)
        idb_t = sb("idb", [128, 128], BF16)
        ones_t = sb("ones", [128, 128], BF16)
        eps_t = sb("eps", [128, 1], F32)
        one_t = sb("one", [128, 1], F32)
        hc_f = sb("hc_f", [128, 3, 128], F32)
        hc_b = sb("hc_b", [128, 2, 128], BF16)
        hc_z = sb("hc_z", [128, 2], F32)
        stat = sb("stat", [128, 64], F32)
        TF_t = sb("TF", [128, 2, 512], F32)
        TB_t = sb("TB", [128, 2, 512], BF16)
        RBYTES = 64 * 1024
        R_t = sb("R", [128, RBYTES // 4], F32)
        ps_all = es.enter_context(nc.psum_tensor("psall", [128, 8, 512], F32))
        ps_t = [ps_all[:, i, :] for i in range(8)]

        def carve(off, shape, dt, pat=None, **kw):
            n = int(np.prod(shape))
            nb = n * (4 if dt == F32 else 2)
            assert off % 4 == 0 and off + nb <= RBYTES, (off, nb)
            ap = R_t[:, off // 4:(off + nb + 3) // 4]
            if dt == BF16:
                ap = ap.bitcast(BF16)
            if len(shape) == 2:
                ap = ap.rearrange("p (a b) -> p a b", a=shape[0])
            elif len(shape) == 3:
                ap = ap.rearrange("p (a b c) -> p a b c", a=shape[0], b=shape[1])
            return ap

        ident = V(ctab[:, 0:128], ["ctab_id"])
        identb = V(idb_t[:], ["idb"])
        onesb = V(ones_t[:], ["ones"])
        epsv = V(eps_t[:, 0:1], ["eps"])

        PSKEYS = {b: [("ps", b)] for b in range(8)}

        def PS(b, cols=slice(0, 512)):
            return V(ps_t[b][:, cols], PSKEYS[b])

        def PSb(b):
            return ps_t[b].bitcast(BF16)

        def TF(i):
            return V(TF_t[:, i, :], [("TF", i)])

        def TB(i):
            return V(TB_t[:, i, :], [("TB", i)])

        def xT(k, tb):
            return V(xT_t[:, k, tb * 512:(tb + 1) * 512], [("xT", k, tb)])

        def uT(k, tb):
            return V(uT_t[:, k, tb * 512:(tb + 1) * 512], [("uT", k, tb)])

        def uTc(k, c):
            return V(uT_t[:, k, c * 128:(c + 1) * 128], [("uT", k, c // 4)])

        ring_ctr = [0]

        def ring_cols(w2d, parts):
            s = ring_ctr[0] % 4
            ring_ctr[0] += 1
            view = ring_t[:, s, :].rearrange("p (k n) -> p k n", k=8)
            src = w2d.rearrange("(k p) n -> p k n", p=128)
            keys = []
            allk = [("ring", s, i) for i in range(2)]
            for i, (d0, s0, n) in enumerate(parts):
                key = ("ring", s, i)
                wk = [key] if len(parts) == 2 else allk
                B.dma("pool", V(view[:, :, d0:d0 + n], wk), src[:, :, s0:s0 + n], key=key)
            return view, allk

        def ring_rows(w2d, r0):
            s = ring_ctr[0] % 4
            ring_ctr[0] += 1
            view = ring_t[:, s, :].rearrange("p (k n) -> p k n", k=4)
            src = w2d[r0:r0 + 512, :].rearrange("(k p) n -> p k n", p=128)
            allk = [("ring", s, i) for i in range(2)]
            B.dma("pool", V(view, allk), src, key=("ring", s, 0))
            return view, allk

        B.dma("sp", V(ctab[:, 0:128], ["ctab_id"]), ctab_d[:, 0:128], key="ctab")
        B.dma("sp", V(gains[:], ["gains"]), gains_d, key="gains")
        B.dma("sp", V(qkg[:], ["qkg"]), qkg_d, key="qkg")
        B.dma("sp", V(dec[:], ["dec"]), dec_d, key="dec")
        B.cp("dve", identb, ident)
        B.memset("dve", onesb, 1.0)
        B.memset("dve", epsv, 1e-6)
        B.memset("dve", V(one_t[:, 0:1], ["one"]), 1.0)
        B.act(V(lg[:], ["lg"]), V(dec[:], ["dec"]), AF.Exp)
        B.ts("dve", V(lg[:], ["lg"]), V(lg[:], ["lg"]), -1.0, None, ALU.mult)
        B.act(V(gc[:], ["gc"]), V(lg[:], ["lg"]), AF.Exp, scale=float(C))
        T_P1, T_M1, T_P2, T_M2, T_IP1, T_CI = [V(ctab[:, (i + 1) * 128:(i + 2) * 128], ["ctab_t"]) for i in range(6)]
        T_c1 = V(ctab[:, 896:897], ["ctab_t"])
        T_c2 = V(ctab[:, 897:898], ["ctab_t"])

        def load_x(s):
            S.barrier()
            for c in range(16):
                xi = V(R_t[:, (c % 2) * 1024:(c % 2 + 1) * 1024], [("xin", c % 2)])
                B.dma("sp", xi, x_d[s, c * 128:(c + 1) * 128, :], key=("xin", c % 2))
                for hb in range(2):
                    pb = PS(4 + (c * 2 + hb) % 4)
                    for j in range(4):
                        k = hb * 4 + j
                        B.tr(V(pb.ap[:, j * 128:(j + 1) * 128], pb.keys),
                             V(xi.ap[:, k * 128:(k + 1) * 128], xi.keys), ident)
                    dst = V(xT_t[:, hb * 4:(hb + 1) * 4, c * 128:(c + 1) * 128],
                            [("xT", hb * 4 + j, c // 4) for j in range(4)])
                    src = V(pb.ap.rearrange("p (a b) -> p a b", a=4), pb.keys)
                    B.cp("act" if hb == 0 else "dve", dst, src)

        def norm(gi):
            for tb in range(4):
                pb = PS(tb % 2)
                for k in range(8):
                    sq = TB(k % 2)
                    B.act(sq, xT(k, tb), AF.Square)
                    B.mm(pb, onesb, sq, start=(k == 0), stop=(k == 7))
                rs = TF(tb % 2)
                B.act(rs, pb, AF.Sqrt, scale=1.0 / D, bias=epsv)
                B.recip(rs, rs)
                for k in range(8):
                    B.stt("dve", uT(k, tb), xT(k, tb), V(gains[:, gi * 8 + k:gi * 8 + k + 1], ["gains"]), rs,
                          ALU.mult, ALU.mult)

        def ret_pre():
            return (ring_cols(wr_in, [(0, 0, 256), (256, 1024, 256)]), ring_cols(wr_in, [(0, 2048, 512)]))

        def retention(pre=None):
            S.barrier()
            B.dma("sp", V(ctab[:, 128:NCT], ["ctab_t"]), ctab_d[:, 128:NCT], key="ctab_t")
            o = 0
            Y_ap = carve(o, [16, 512], BF16); o += 16384
            Ytab = R_t[:, 0:4096]
            qT_ap = carve(o, [2, SEQ], BF16); o += 8192
            kT_ap = carve(o, [2, SEQ], BF16); o += 8192
            v_ap = carve(o, [16, 512], BF16); o += 16384
            TF23 = carve(o, [2, 512], F32); o += 4096
            qx_ap = carve(o, [2, 2, 128], BF16); o += 1024
            kz_ap = carve(o, [2, 256], BF16); o += 1024
            adt_ap = carve(o, [2, 128], BF16); o += 512
            Sst_ap = carve(o, [2, 2, 512], BF16); o += 4096
            ygT_ap = carve(o, [4, 512], BF16); o += 4096

            def Ytabv(which, tb):
                base = which * 2048 + tb * 512
                kb = which * 8 + tb * 2
                return V(Ytab[:, base:base + 512], [("Y", kb), ("Y", kb + 1)])

            def yb(c):
                return V(Y_ap[:, c, :], [("Y", c)])

            def qTv(m, tb):
                return V(qT_ap[:, m, tb * 512:(tb + 1) * 512], [("qT", tb)])

            def kTv(m, tb):
                return V(kT_ap[:, m, tb * 512:(tb + 1) * 512], [("kT", tb)])

            def vv(c):
                return V(v_ap[:, c, :], [("v", c)])

            TFx = [TF(0), TF(1), V(TF23[:, 0, :], [("TF", 2)]), V(TF23[:, 1, :], [("TF", 3)])]
            bankctr = [0]

            for h in range(4):
                lgf = V(lg[:, h:h + 1], ["lg"])
                lgb = V(lg[:, 4 + h:5 + h], ["lg"])
                DT = V(hc_f[:, 0, :], ["hcDT"])
                xiF = V(hc_f[:, 1, :], ["hcxiF"])
                xiB = V(hc_f[:, 2, :], ["hcxiB"])
                gIf = V(hc_b[:, 0, :], ["hcgIf"])
                gIb = V(hc_b[:, 1, :], ["hcgIb"])
                zf = V(hc_z[:, 0:1], ["hczf"])
                zb = V(hc_z[:, 1:2], ["hczb"])
                t0, t1 = TFx[2], TFx[3]
                t0s = V(t0.ap[:, 0:128], t0.keys)
                t1s = V(t1.ap[:, 0:128], t1.keys)
                B.act(t0s, T_P1, AF.Exp, scale=lgf)
                B.act(t1s, T_P2, AF.Exp, scale=lgb)
                B.tt("dve", t0s, t0s, T_M1, ALU.mult)
                B.tt("dve", t1s, t1s, T_M2, ALU.mult)
                B.tt("dve", t0s, t0s, t1s, ALU.add)
                B.ts("dve", DT, t0s, 1.0 / 16.0, None, ALU.mult)
                B.act(xiF, T_IP1, AF.Exp, scale=lgf)
                B.act(xiB, T_CI, AF.Exp, scale=lgb)
                B.act(zf, T_c1, AF.Exp, scale=lgf)
                B.act(zb, T_c2, AF.Exp, scale=lgb)
                B.ts("dve", zf, zf, 1.0 / 16.0, None, ALU.mult)
                B.ts("dve", zb, zb, 1.0 / 16.0, None, ALU.mult)
                gcf = V(gc[:, h:h + 1], ["gc"])
                gcb = V(gc[:, 4 + h:5 + h], ["gc"])

                if pre is not None:
                    (wqk, kqk), (wv, kwv) = pre
                    pre = None
                else:
                    wqk, kqk = ring_cols(wr_in, [(0, h * 256, 256), (256, 1024 + h * 256, 256)])
                    wv, kwv = ring_cols(wr_in, [(0, 2048 + h * 512, 512)])
                B.dma("sp", V(Ytab[:, 0:2048], [("Y", i) for i in range(8)]), cosr_d, key="tabc")
                B.dma("sp", V(Ytab[:, 2048:4096], [("Y", i) for i in range(8, 16)]), sinr_d, key="tabs")

                for which, dstv in ((0, qTv), (1, kTv)):
                    for tp in range(2):
                        base = (bankctr[0] % 2) * 4
                        bankctr[0] += 1
                        for m in range(2):
                            for k in range(8):
                                for j in range(2):
                                    B.mm(PS(base + m * 2 + j),
                                         V(wqk[:, k, which * 256 + m * 128: which * 256 + (m + 1) * 128], kqk),
                                         uT(k, tp * 2 + j), start=(k == 0), stop=(k == 7))
                        for j in range(2):
                            tb = tp * 2 + j
                            q1, q2 = PS(base + j), PS(base + 2 + j)
                            cs, sn = Ytabv(0, tb), Ytabv(1, tb)
                            B.tt("dve", TFx[0], q1, cs, ALU.mult)
                            B.tt("dve", TFx[1], q2, sn, ALU.mult)
                            B.tt("pool", dstv(0, tb), TFx[0], TFx[1], ALU.subtract)
                            B.tt("dve", TFx[2], q1, sn, ALU.mult)
                            B.tt("dve", TFx[3], q2, cs, ALU.mult)
                            B.tt("pool", dstv(1, tb), TFx[2], TFx[3], ALU.add)
                for c in range(16):
                    pb = PS(c % 4)
                    for k in range(8):
                        B.mm(pb, uTc(k, c), V(wv[:, k, :], kwv), start=(k == 0), stop=(k == 7))
                    B.cp("act", vv(c), pb)
                wg, kwg = ring_cols(wr_in, [(0, 4096 + h * 512, 512)])
                wo, kwo = ring_rows(wr_out, h * 512)
                if h < 3:
                    pre = (ring_cols(wr_in, [(0, (h + 1) * 256, 256), (256, 1024 + (h + 1) * 256, 256)]),
                           ring_cols(wr_in, [(0, 2048 + (h + 1) * 512, 512)]))

                def kz_make(c, zcol, buf):
                    pbk = PSb(7)
                    pv = V(pbk[:, 0:256], PSKEYS[7])
                    for m in range(2):
                        B.tr(V(pbk[:, m * 128:(m + 1) * 128], PSKEYS[7]),
                             V(kT_ap[:, m, c * 128:(c + 1) * 128], [("kT", c // 4)]), identb)
                    kz = V(kz_ap[:, buf, :], [("kz", buf)])
                    B.ts("dve", kz, pv, zcol, None, ALU.mult)
                    return kz

                def state_update(c, kz, gcol, cur, nxt, first):
                    for m in range(2):
                        pb = PS(5 + m)
                        B.mm(pb, V(kz.ap[:, m * 128:(m + 1) * 128], kz.keys), vv(c), start=True, stop=True)
                        dst = V(Sst_ap[:, nxt, m, :], [("S", nxt, m)])
                        if first:
                            B.cp("act", dst, pb)
                        else:
                            B.stt("dve", dst, V(Sst_ap[:, cur, m, :], [("S", cur, m)]), gcol, pb, ALU.mult, ALU.add)

                def Sv(cur, m):
                    return V(Sst_ap[:, cur, m, :], [("S", cur, m)])

                def qx_make(c, xi, buf):
                    qx = V(qx_ap[:, buf, :, :], [("qx", buf)])
                    src = V(qT_ap[:, :, c * 128:(c + 1) * 128], [("qT", c // 4)])
                    xib = V(xi.ap.unsqueeze(1).to_broadcast([128, 2, 128]), xi.keys)
                    B.tt("pool", qx, src, xib, ALU.mult)
                    return qx

                cur = 0
                kzs = {15: kz_make(15, zb, 1)}
                for c in range(15, -1, -1):
                    if c - 1 >= 1:
                        kzs[c - 1] = kz_make(c - 1, zb, (c - 1) % 2)
                    if c < 15:
                        qx = qx_make(c, xiB, c % 2)
                        pb = PS(4)
                        for m in range(2):
                            B.mm(pb, V(qx.ap[:, m, :], qx.keys), Sv(cur, m), start=(m == 0), stop=(m == 1))
                        B.cp("act", yb(c), pb)
                    if c > 0:
                        state_update(c, kzs[c], gcb, cur, 1 - cur, first=(c == 15))
                        cur = 1 - cur

                onev = V(one_t[:, 0:1], ["one"])
                fstate = {"cur": 0}
                pbt = PSb(7)

                def stage_A(c):
                    tb = c // 4
                    cur = fstate["cur"]
                    if c < 15:
                        kz = kz_make(c, zf, c % 2)
                    pa = V(ps_t[4][:, 0:128], PSKEYS[4])
                    for m in range(2):
                        B.mm(pa, V(kT_ap[:, m, c * 128:(c + 1) * 128], [("kT", tb)]),
                             V(qT_ap[:, m, c * 128:(c + 1) * 128], [("qT", tb)]), start=(m == 0), stop=(m == 1))
                    adt = V(adt_ap[:, c % 2, :], [("adt", c % 2)])
                    B.tt("dve", adt, pa, DT, ALU.mult)
                    if c > 0:
                        qx = qx_make(c, xiF, c % 2)
                    if c < 15:
                        nxt = 1 - cur
                        state_update(c, kz, gcf, cur, nxt, first=(c == 0))
                        fstate["cur"] = nxt
                    py = PS(c % 2)
                    mlist = [(adt, vv(c))]
                    if c > 0:
                        for m in range(2):
                            mlist.append((V(qx.ap[:, m, :], qx.keys), Sv(cur, m)))
                    if c < 15:
                        mlist.append((identb, yb(c)))
                    for mi, (l_, r_) in enumerate(mlist):
                        B.mm(py, l_, r_, start=(mi == 0), stop=(mi == len(mlist) - 1))
                    pg = PS(2 + c % 2)
                    for k in range(8):
                        B.mm(pg, uTc(k, c), V(wg[:, k, :], kwg), start=(k == 0), stop=(k == 7))

                SG = [TFx[1], TFx[3]]

                def gate(c):
                    sgc = SG[c % 2]
                    B.act(sgc, PS(2 + c % 2), AF.Exp, scale=-1.0)
                    B.act(sgc, sgc, AF.Ln, bias=onev)
                    B.act(sgc, sgc, AF.Exp, scale=-1.0)

                def stage_B(c):
                    tb = c // 4
                    py = PS(c % 2)
                    pg = PS(2 + c % 2)
                    st6 = V(stat[:, 0:6], ["st6"])
                    mv = V(stat[:, 8:10], ["mv"])
                    rs = V(stat[:, 10:11], ["rs"])
                    S.op("dve", lambda e, a=st6.ap, b=py.ap: e.bn_stats(a, b), reads=_k(py), writes=_k(st6))
                    S.op("dve", lambda e, a=mv.ap, b=st6.ap: e.bn_aggr(a, b), reads=_k(st6), writes=_k(mv))
                    B.act(rs, V(stat[:, 9:10], ["mv"]), AF.Ln, bias=epsv)
                    B.act(rs, rs, AF.Exp, scale=-0.5)
                    sg = SG[c % 2]
                    if c + 1 < 16:
                        gate(c + 1)
                    yn = TFx[0]
                    B.ts("dve", yn, py, V(stat[:, 8:9], ["mv"]), rs, ALU.subtract, ALU.mult)
                    B.tt("dve", sg, pg, sg, ALU.mult)
                    ygn = TB(c % 2)
                    B.tt("dve", ygn, yn, sg, ALU.mult)
                    for j in range(4):
                        B.tr(V(pbt[:, 256 + j * 128: 256 + (j + 1) * 128], PSKEYS[7]),
                             V(ygn.ap[:, j * 128:(j + 1) * 128], ygn.keys), identb)
                    B.cp("act", V(ygT_ap[:, :, (c % 4) * 128:(c % 4 + 1) * 128], [("ygT",)]),
                         V(pbt[:, 256:768].rearrange("p (a b) -> p a b", a=4), PSKEYS[7]))
                    if c % 4 == 3:
                        obanks = [4, c % 2, 2 + c % 2]
                        for m in range(8):
                            pb = PS(obanks[m % 3])
                            for j in range(4):
                                B.mm(pb, V(wo[:, j, m * 128:(m + 1) * 128], kwo),
                                     V(ygT_ap[:, j, :], [("ygT",)]), start=(j == 0), stop=(j == 3))
                            B.tt("dve", xT(m, tb), xT(m, tb), pb, ALU.add)

                for i in range(17):
                    if i < 16:
                        stage_A(i)
                    if i == 0:
                        gate(0)
                    if i >= 1:
                        stage_B(i - 1)

        def mlp_pre(li):
            return ([ring_cols(w1_d[li], [(0, i * 512, 512)]) for i in range(2)],
                    [ring_rows(w2_d[li], i * 512) for i in range(2)])

        def mlp(li, pre=None):
            S.barrier()
            h1_ap = [carve(i * 32768, [8, SEQ], BF16) for i in range(2)]
            w1 = w1_d[li]
            w2 = w2_d[li]
            gctr = 0
            for q in range(4):
                hb = q % 2
                if q == 0 and pre is not None:
                    ws, ws2 = pre
                else:
                    ws = [ring_cols(w1, [(0, q * 1024 + i * 512, 512)]) for i in range(2)]
                    ws2 = [ring_rows(w2, q * 1024 + i * 512) for i in range(2)]
                for m in range(8):
                    wv_, kw_ = ws[m // 4]
                    col = (m % 4) * 128
                    base = (gctr % 2) * 4
                    gctr += 1
                    for k in range(8):
                        for t in range(4):
                            B.mm(PS(base + t), V(wv_[:, k, col:col + 128], kw_), uT(k, t),
                                 start=(k == 0), stop=(k == 7))
                    for t in range(4):
                        r = TF(t % 2)
                        B.act(r, PS(base + t), AF.Relu)
                        B.tt("pool" if t % 2 == 0 else "dve",
                             V(h1_ap[hb][:, m, t * 512:(t + 1) * 512], [("h1", hb, m, t)]), r, r, ALU.mult)
                for m in range(8):
                    base = (gctr % 2) * 4
                    gctr += 1
                    for kk in range(8):
                        wv_, kw_ = ws2[kk // 4]
                        for t in range(4):
                            B.mm(PS(base + t), V(wv_[:, kk % 4, m * 128:(m + 1) * 128], kw_),
                                 V(h1_ap[hb][:, kk, t * 512:(t + 1) * 512], [("h1", hb, kk, t)]),
                                 start=(kk == 0), stop=(kk == 7))
                    for t in range(4):
                        B.tt("dve", xT(m, t), xT(m, t), PS(base + t), ALU.add)

        def attn_pre():
            return [ring_cols(wa_in, [(0, i * 512, 512)]) for i in range(3)]

        def attention(pre=None):
            S.barrier()
            o = 0
            cos_ap = carve(o, [16, 64], F32); o += 4096
            sin_ap = carve(o, [16, 64], F32); o += 4096
            qT_ap = carve(o, [8, SEQ], BF16); o += 32768
            kT_ap = carve(o, [2, SEQ], BF16); o += 8192
            v_ap = carve(o, [16, 256], BF16); o += 8192
            G = o
            qn_ap = carve(G, [10, 128], F32)
            r1_ap = carve(G + 5120, [10, 64], F32)
            r2_ap = TF_t[:].rearrange("p a b -> p (a b)")[:, 0:640].rearrange("p (h d) -> p h d", d=64)
            qr_ap = ctab[:, 128:768].bitcast(BF16).rearrange("p (h d) -> p h d", d=128)
            PT_ap = carve(G, [3, 512], BF16)
            B.dma("sp", V(cos_ap, ["cosa"]), cosa_d.rearrange("(c p) f -> p c f", p=128), key="cosa")
            B.dma("sp", V(sin_ap, ["sina"]), sina_d.rearrange("(c p) f -> p c f", p=128), key="sina")
            ws = pre if pre is not None else [ring_cols(wa_in, [(0, i * 512, 512)]) for i in range(3)]
            qg = V(qkg[:, 0:128].unsqueeze(1).to_broadcast([128, 8, 128]), ["qkg"])
            kg = V(qkg[:, 128:256].unsqueeze(1).to_broadcast([128, 2, 128]), ["qkg"])
            QN, R1, R2, QR = [("qn",)], [("r1",)], [("TF", 0), ("TF", 1)], [("qr",)]

            def inproj_mm(c):
                base = (c % 2) * 3
                for k in range(8):
                    for n in range(3):
                        B.mm(PS(base + n), uTc(k, c), V(ws[n][0][:, k, :], ws[n][1]), start=(k == 0), stop=(k == 7))

            sqq = V(TB_t[:].rearrange("p a (h d) -> p (a h) d", d=128), [("sqq",)])
            sqk = V(hc_b[:], [("sqk",)])

            def stage1(c):
                base = (c % 2) * 3
                pq = V(ps_all[:, base:base + 2, :].rearrange("p a (h d) -> p (a h) d", d=128),
                       PSKEYS[base] + PSKEYS[base + 1])
                pk = V(ps_t[base + 2][:, 0:256].rearrange("p (h d) -> p h d", d=128), PSKEYS[base + 2])
                B.cp("act", V(v_ap[:, c, :], [("v", c)]), PS(base + 2, slice(256, 512)))
                sc = 32 + (c % 2) * 16
                ssq = V(stat[:, sc:sc + 8], [("ssc", c % 2, hh) for hh in range(8)])
                ssk = V(stat[:, sc + 8:sc + 10], [("ssc", c % 2, hh) for hh in range(8, 10)])
                B.act(sqq, pq, AF.Square)
                B.act(sqk, pk, AF.Square)
                S.op("dve", lambda e, o_=ssq.ap, i_=sqq.ap: e.reduce_sum(o_, i_, axis=AX.X), reads=_k(sqq), writes=_k(ssq))
                S.op("dve", lambda e, o_=ssk.ap, i_=sqk.ap: e.reduce_sum(o_, i_, axis=AX.X), reads=_k(sqk), writes=_k(ssk))
                ss = V(stat[:, sc:sc + 10], list(ssq.keys) + list(ssk.keys))
                B.act(ss, ss, AF.Sqrt, scale=1.0 / 128, bias=epsv)

            def stage2a(c):
                base = (c % 2) * 3
                sc = 32 + (c % 2) * 16
                pkeys = PSKEYS[base] + PSKEYS[base + 1] + PSKEYS[base + 2]
                pqk = V(ps_all[:, base:base + 3, :].rearrange("p a b -> p (a b)")[:, 0:1280].rearrange(
                    "p (h d) -> p h d", d=128), pkeys)
                sskeys = [("ssc", c % 2, hh) for hh in range(10)]
                ssv = V(stat[:, sc:sc + 10], sskeys)
                B.recip(ssv, ssv)
                qn = V(qn_ap, QN)
                rq = V(stat[:, sc:sc + 10].unsqueeze(2).to_broadcast([128, 10, 128]), sskeys)
                B.tt("dve", qn, pqk, rq, ALU.mult)

            def stage2b(c):
                B.tt("dve", V(qn_ap[:, 0:8, :], QN), V(qn_ap[:, 0:8, :], QN), qg, ALU.mult)
                B.tt("pool", V(qn_ap[:, 8:10, :], QN), V(qn_ap[:, 8:10, :], QN), kg, ALU.mult)
                cosb = V(cos_ap[:, c, :].unsqueeze(1).to_broadcast([128, 10, 64]), ["cosa"])
                sinb = V(sin_ap[:, c, :].unsqueeze(1).to_broadcast([128, 10, 64]), ["sina"])
                x1 = V(qn_ap[:, :, 0:64], QN)
                x2 = V(qn_ap[:, :, 64:128], QN)
                r1 = V(r1_ap, R1)
                r2 = V(r2_ap, R2)
                B.tt("dve", r1, x1, cosb, ALU.mult)
                B.tt("dve", r2, x2, sinb, ALU.mult)
                B.tt("dve", V(qr_ap[:, :, 0:64], QR), r1, r2, ALU.subtract)
                B.tt("dve", r1, x1, sinb, ALU.mult)
                B.tt("dve", r2, x2, cosb, ALU.mult)
                B.tt("dve", V(qr_ap[:, :, 64:128], QR), r1, r2, ALU.add)
                pbq = PSb(6)
                for hh in range(8):
                    B.tr(V(pbq[:, hh * 128:(hh + 1) * 128], PSKEYS[6]), V(qr_ap[:, hh, :], QR), identb)
                pbk = PSb(7)
                for hh in range(2):
                    B.tr(V(pbk[:, hh * 128:(hh + 1) * 128], PSKEYS[7]), V(qr_ap[:, 8 + hh, :], QR), identb)
                B.cp("act", V(qT_ap[:, 0:8, c * 128:(c + 1) * 128], [("qTa", i, c // 4) for i in range(8)]),
                     V(pbq[:, 0:1024].rearrange("p (a b) -> p a b", a=8), PSKEYS[6]))
                B.cp("act", V(kT_ap[:, 0:2, c * 128:(c + 1) * 128], [("kTa", i) for i in range(2)]),
                     V(pbk[:, 0:256].rearrange("p (a b) -> p a b", a=2), PSKEYS[7]))

            inproj_mm(0)
            inproj_mm(1)
            stage1(0)
            for c in range(16):
                if c + 1 < 16:
                    stage1(c + 1)
                stage2a(c)
                if c + 2 < 16:
                    inproj_mm(c + 2)
                stage2b(c)
            S.barrier()
            wo = [ring_rows(wa_out, i * 512) for i in range(2)]
            scale = 128.0 ** -0.5
            it = 0
            for kv in range(2):
                for g in range(4):
                    h = kv * 4 + g
                    for qb in range(4):
                        po = PS(3 + it % 2)
                        pd = PS(5 + it % 2)
                        it += 1
                        qv = V(qT_ap[:, h, qb * 512:(qb + 1) * 512], [("qTa", h, qb)])
                        for i in range(16 + 2):
                            if i < 16:
                                pS = PS(i % 3)
                                B.mm(pS, V(kT_ap[:, kv, i * 128:(i + 1) * 128], [("kTa", kv)]), qv, start=True, stop=True)
                                B.act(V(PT_ap[:, i % 3, :], [("PT", i % 3)]), pS, AF.Exp, scale=scale)
                            if i >= 2:
                                kt = i - 2
                                pt = V(PT_ap[:, kt % 3, :], [("PT", kt % 3)])
                                B.mm(po, V(v_ap[:, kt, kv * 128:(kv + 1) * 128], [("v", kt)]), pt,
                                     start=(kt == 0), stop=(kt == 15))
                                B.mm(pd, onesb, pt, start=(kt == 0), stop=(kt == 15))
                        rd = TF(it % 2)
                        B.recip(rd, pd)
                        B.tt("dve", uT(h, qb), po, rd, ALU.mult)
            gctr = 0
            for m in range(8):
                base = (gctr % 2) * 4
                gctr += 1
                for k in range(8):
                    for t in range(4):
                        B.mm(PS(base + t),
                             V(wo[k // 4][0][:, k % 4, m * 128:(m + 1) * 128], wo[k // 4][1]), uT(k, t),
                             start=(k == 0), stop=(k == 7))
                for t in range(4):
                    B.tt("dve", xT(m, t), xT(m, t), PS(base + t), ALU.add)

        def final(s):
            S.barrier()
            gf = V(R_t[:, 0:1024], ["gfin"])
            B.dma("sp", gf, gfin_d, key="gfin")
            ob = [V(R_t[:, 1024 * (i + 1):1024 * (i + 2)], [("ob", i)]) for i in range(2)]
            junk = V(R_t[:, 3072:3584], ["junk"])
            for c in range(16):
                obv = ob[c % 2]
                pbs = [PS((c % 2) * 2 + hb) for hb in range(2)]
                for hb in range(2):
                    for j in range(4):
                        k = hb * 4 + j
                        B.tr(V(pbs[hb].ap[:, j * 128:(j + 1) * 128], pbs[hb].keys),
                             V(xT_t[:, k, c * 128:(c + 1) * 128], [("xT", k, c // 4)]), ident)
                ssv = V(stat[:, 0:2], ["fss"])
                B.memset("dve", ssv, 0.0)
                for hb in range(2):
                    B.act(junk, pbs[hb], AF.Square, accum=V(stat[:, hb:hb + 1], ["fss"]))
                rs = V(stat[:, 2:3], ["frs"])
                B.tt("dve", rs, V(stat[:, 0:1], ["fss"]), V(stat[:, 1:2], ["fss"]), ALU.add)
                B.act(rs, rs, AF.Sqrt, scale=1.0 / D, bias=epsv)
                B.recip(rs, rs)
                for hb in range(2):
                    B.stt("dve", V(obv.ap[:, hb * 512:(hb + 1) * 512], obv.keys), pbs[hb], rs,
                          V(gf.ap[:, hb * 512:(hb + 1) * 512], gf.keys), ALU.mult, ALU.mult)
                B.dma("sp", y_d[s, c * 128:(c + 1) * 128, :], obv, key=("out", c % 2))
            for i in range(2):
                if ("out", i) not in S.final_waits:
                    S.final_waits.append(("out", i))

        for s in range(nseq):
            load_x(s)
            if "ret" in stages:
                p_ = ret_pre()
                norm(0)
                retention(p_)
            if "mlp0" in stages:
                p_ = mlp_pre(0)
                norm(1)
                mlp(0, p_)
            if "attn" in stages:
                p_ = attn_pre()
                norm(2)
                attention(p_)
            if "mlp1" in stages:
                p_ = mlp_pre(1)
                norm(3)
                mlp(1, p_)
            final(s)
        S.emit()
    return nc


def _rope_tables(seq, rot_dim):
    rows = seq // 64
    row = np.broadcast_to(np.arange(rows, dtype=np.float32)[:, None], (rows, 64)).reshape(seq)
    col = np.broadcast_to(np.arange(64, dtype=np.float32)[None, :], (rows, 64)).reshape(seq)
    per_axis = rot_dim // 2
    n_freq = per_axis // 2
    inv_freq = (np.float32(10000.0) ** (-np.arange(n_freq, dtype=np.float32) * np.float32(2.0) / np.float32(per_axis))).astype(np.float32)
    ang = np.concatenate([row[:, None] * inv_freq[None, :], col[:, None] * inv_freq[None, :]], axis=-1).astype(np.float32)
    return np.cos(ang).astype(np.float32), np.sin(ang).astype(np.float32)


def _const_table():
    j = np.arange(128, dtype=np.float32)[:, None]
    i = np.arange(128, dtype=np.float32)[None, :]
    t = np.zeros((128, NCT), np.float32)
    t[:, 0:128] = np.eye(128, dtype=np.float32)
    t[:, 128:256] = np.maximum(i - j, 0)
    t[:, 256:384] = (i >= j)
    t[:, 384:512] = np.maximum(j - i, 0)
    t[:, 512:640] = (j > i)
    t[:, 640:768] = np.broadcast_to(i + 1.0, (128, 128))
    t[:, 768:896] = np.broadcast_to(128.0 - i, (128, 128))
    t[:, 896] = 127.0 - j[:, 0]
    t[:, 897] = j[:, 0]
    return t


def make_in_maps(inputs, nseq=NSEQ, ncores=NCORES):
    f = lambda a: np.ascontiguousarray(np.asarray(a, dtype=np.float32))
    x = f(inputs["x"])
    cos_r, sin_r = _rope_tables(SEQ, 256)
    cos_a, sin_a = _rope_tables(SEQ, 128)
    nm, nl = f(inputs["norm_mix"]), f(inputs["norm_mlp"])
    gl = [nm[0], nl[0], nm[1], nl[1]]
    gains = np.concatenate([g.reshape(8, 128).T for g in gl], axis=1)
    common = {
        "ret_w_in": f(inputs["ret_w_in"][0]), "ret_w_out": f(inputs["ret_w_out"][0]),
        "mlp_w1": f(inputs["mlp_w1"]), "mlp_w2": f(inputs["mlp_w2"]),
        "attn_w_in": f(inputs["attn_w_in"][0]), "attn_w_out": f(inputs["attn_w_out"][0]),
        "ctab": _const_table(), "gains": f(gains),
        "gfinal": f(np.broadcast_to(f(inputs["final_norm"])[None, :], (128, D))),
        "qkg": f(np.broadcast_to(np.concatenate([f(inputs["attn_q_norm"][0]), f(inputs["attn_k_norm"][0])])[None, :], (128, 256))),
        "dec": f(np.broadcast_to(np.concatenate([f(inputs["ret_decay_fwd"][0]), f(inputs["ret_decay_bwd"][0])])[None, :], (128, 8))),
        "cos_r": f(cos_r.T), "sin_r": f(sin_r.T), "cos_a": f(cos_a), "sin_a": f(sin_a),
    }
    maps = []
    for c in range(ncores):
        m = dict(common)
        m["x"] = f(x[c * nseq:(c + 1) * nseq])
        maps.append(m)
    return maps


_NC_CACHE = {}


def kernel(**inputs):
    if "nc" not in _NC_CACHE:
        _NC_CACHE["nc"] = build()
    nc = _NC_CACHE["nc"]
    in_maps = make_in_maps(inputs)
    res = run_bass_kernel_spmd(nc, in_maps, core_ids=list(range(NCORES)))
    out = np.concatenate([np.asarray(r["y"]) for r in res.results], axis=0)
    return out.astype(np.float32)
```

```python
import contextlib
import numpy as np
import concourse.bass as bass
import concourse.mybir as mybir
from concourse.bass_utils import run_bass_kernel_spmd

F32 = mybir.dt.float32
BF16 = mybir.dt.bfloat16
AF = mybir.ActivationFunctionType
ALU = mybir.AluOpType
AX = mybir.AxisListType

NCORES = 8
D = 1024
SEQ = 2048
NSEQ = 2
C = 128
ENGS = ("pe", "act", "dve", "pool", "sp")


class Op:
    __slots__ = ("eng", "fn", "deps", "dma", "signal", "tick", "sem", "idx")

    def __init__(self, eng, fn, dma):
        self.eng = eng
        self.fn = fn
        self.deps = set()
        self.dma = dma
        self.signal = False
        self.tick = 0
        self.sem = None


class Sched:
    def __init__(self, nc):
        self.nc = nc
        self.ops = []
        self.state = {}
        self.final_waits = []
        self.last_on = {}
        self.last_dma = {}
        self.bar = ()

    def barrier(self):
        self.bar = tuple(set(self.last_on.values()) | set(self.last_dma.values()))

    def op(self, eng, fn, reads=(), writes=(), dma=False, dma_key=None):
        o = Op(eng, fn, dma)
        idx = len(self.ops)
        o.idx = idx
        o.deps.update(self.bar)
        for r in reads:
            st = self.state.get(r)
            if st is None:
                st = self.state[r] = [None, {}, []]
            if st[0] is not None:
                o.deps.add(st[0])
            if dma:
                st[2].append(idx)
            else:
                st[1][eng] = idx
        for w in writes:
            st = self.state.get(w)
            if st is None:
                st = self.state[w] = [None, {}, []]
            if st[0] is not None:
                o.deps.add(st[0])
            for r in st[1].values():
                if r != idx:
                    o.deps.add(r)
            for r in st[2]:
                if r != idx:
                    o.deps.add(r)
            st[0] = idx
            st[1] = {}
            st[2] = []
        if dma:
            o.sem = dma_key
            self.last_dma[dma_key] = idx
        self.ops.append(o)
        self.last_on[eng] = idx
        return idx

    def emit(self):
        nc = self.nc
        ops = self.ops
        per_eng = {e: [o for o in ops if o.eng == e] for e in ENGS}
        dma_counts = {}
        for e in ENGS:
            for i, o in enumerate(per_eng[e]):
                o.tick = i + 1
        for o in ops:
            if o.dma:
                dma_counts[o.sem] = dma_counts.get(o.sem, 0) + 16
                o.tick = dma_counts[o.sem]
        waits = {}
        for ename in ENGS:
            seen = {}
            for o in per_eng[ename]:
                need = {}
                for d in o.deps:
                    od = ops[d]
                    if od.dma:
                        key = ("d", od.sem)
                    else:
                        if od.eng == "pe" and ename == "pe":
                            continue
                        key = ("e", od.eng)
                    if key not in need or od.tick > need[key].tick:
                        need[key] = od
                wl = []
                for key, od in need.items():
                    if seen.get(key, 0) >= od.tick:
                        continue
                    seen[key] = od.tick
                    wl.append((key, od))
                    od.signal = True
                waits[o.idx] = wl
        real = {e: 0 for e in ENGS}
        for e in ENGS:
            for o in per_eng[e]:
                if o.dma:
                    o.signal = True
                elif o.signal:
                    real[e] += 1
                    o.tick = real[e]
        dma_keys = list(dma_counts.keys())
        with contextlib.ExitStack() as es:
            esem = {e: es.enter_context(nc.semaphore("s_" + e)) for e in ENGS}
            dsem = {k: es.enter_context(nc.semaphore("d_%d" % i)) for i, k in enumerate(dma_keys)}
            block = es.enter_context(nc.Block())

            def run_engine(ename, eng):
                for o in per_eng[ename]:
                    for key, od in waits[o.idx]:
                        s = dsem[key[1]] if key[0] == "d" else esem[key[1]]
                        eng.wait_ge(s, od.tick)
                    ins = o.fn(eng)
                    if o.signal:
                        if o.dma:
                            ins.then_inc(dsem[o.sem], 16)
                        else:
                            ins.then_inc(esem[ename], 1)
                if ename == "sp":
                    for k in self.final_waits:
                        eng.wait_ge(dsem[k], dma_counts[k])

            @block.tensor
            def _(e):
                run_engine("pe", e)

            @block.scalar
            def _(e):
                run_engine("act", e)

            @block.vector
            def _(e):
                run_engine("dve", e)

            @block.gpsimd
            def _(e):
                run_engine("pool", e)

            @block.sync
            def _(e):
                run_engine("sp", e)


class V:
    __slots__ = ("ap", "keys")

    def __init__(self, ap, keys):
        self.ap = ap
        self.keys = tuple(keys)


def _k(*vs):
    out = []
    for v in vs:
        if isinstance(v, V):
            out.extend(v.keys)
    return out


def _a(v):
    return v.ap if isinstance(v, V) else v


class Builder:
    def __init__(self, nc):
        self.nc = nc
        self.S = Sched(nc)

    def mm(self, out, lhsT, rhs, start, stop):
        self.S.op("pe", lambda e: e.matmul(out.ap, lhsT.ap, rhs.ap, start=start, stop=stop),
                  reads=_k(lhsT, rhs), writes=_k(out))

    def tr(self, out, in_, ident):
        self.S.op("pe", lambda e: e.transpose(out.ap, in_.ap, ident.ap), reads=_k(in_, ident), writes=_k(out))

    def act(self, out, in_, func, scale=1.0, bias=None, accum=None):
        def fn(e):
            kw = {}
            if bias is not None:
                kw["bias"] = _a(bias)
            if accum is not None:
                kw["accum_out"] = _a(accum)
            return e.activation(out.ap, in_.ap, func, scale=_a(scale), **kw)
        self.S.op("act", fn, reads=_k(in_, scale, bias), writes=_k(out, accum))

    def tt(self, eng, out, in0, in1, op):
        self.S.op(eng, lambda e: e.tensor_tensor(out.ap, in0.ap, in1.ap, op=op), reads=_k(in0, in1), writes=_k(out))

    def ts(self, eng, out, in0, s1, s2, op0, op1=None):
        def fn(e):
            if op1 is None:
                return e.tensor_scalar(out.ap, in0.ap, _a(s1), None, op0=op0)
            return e.tensor_scalar(out.ap, in0.ap, _a(s1), _a(s2), op0=op0, op1=op1)
        self.S.op(eng, fn, reads=_k(in0, s1, s2), writes=_k(out))

    def stt(self, eng, out, in0, scalar, in1, op0, op1):
        self.S.op(eng, lambda e: e.scalar_tensor_tensor(out.ap, in0.ap, _a(scalar), in1.ap, op0=op0, op1=op1),
                  reads=_k(in0, scalar, in1), writes=_k(out))

    def cp(self, eng, out, in_):
        if eng == "act":
            self.S.op("act", lambda e: e.copy(out.ap, in_.ap), reads=_k(in_), writes=_k(out))
        else:
            self.S.op(eng, lambda e: e.tensor_copy(out.ap, in_.ap), reads=_k(in_), writes=_k(out))

    def recip(self, out, in_):
        self.S.op("dve", lambda e: e.reciprocal(out.ap, in_.ap), reads=_k(in_), writes=_k(out))

    def memset(self, eng, out, val):
        self.S.op(eng, lambda e: e.memset(out.ap, val), writes=_k(out))

    def dma(self, eng, out, in_, key, reads=(), writes=()):
        self.S.op(eng, lambda e: e.dma_start(out=_a(out), in_=_a(in_)), reads=list(reads) + _k(in_),
                  writes=list(writes) + _k(out), dma=True, dma_key=key)


NCT = 7 * 128 + 2


def build(nseq=NSEQ, stages=("ret", "mlp0", "attn", "mlp1")):
    nc = bass.Bass("TRN2", target_bir_lowering=False)
    B = Builder(nc)
    S = B.S

    def din(name, shape):
        return nc.dram_tensor(name, list(shape), F32, kind="ExternalInput").ap()

    x_d = din("x", [nseq, SEQ, D])
    wr_in = din("ret_w_in", [D, 6144])
    wr_out = din("ret_w_out", [2048, D])
    w1_d = din("mlp_w1", [2, D, 4096])
    w2_d = din("mlp_w2", [2, 4096, D])
    wa_in = din("attn_w_in", [D, 1536])
    wa_out = din("attn_w_out", [D, D])
    ctab_d = din("ctab", [128, NCT])
    gains_d = din("gains", [128, 32])
    gfin_d = din("gfinal", [128, D])
    qkg_d = din("qkg", [128, 256])
    dec_d = din("dec", [128, 8])
    cosr_d = din("cos_r", [128, SEQ])
    sinr_d = din("sin_r", [128, SEQ])
    cosa_d = din("cos_a", [SEQ, 64])
    sina_d = din("sin_a", [SEQ, 64])
    y_d = nc.dram_tensor("y", [nseq, SEQ, D], F32, kind="ExternalOutput").ap()

    es = contextlib.ExitStack()
    with es:
        def sb(name, shape, dt):
            return es.enter_context(nc.sbuf_tensor("sb_" + name, list(shape), dt))

        xT_t = sb("xT", [128, 8, SEQ], F32)
        uT_t = sb("uT", [128, 8, SEQ], BF16)
        ring_t = sb("ring", [128, 4, 4096], BF16)
        ctab = sb("ctab", [128, NCT], F32)
        gains = sb("gains", [128, 32], F32)
        qkg = sb("qkg", [128, 256], F32)
        dec = sb("dec", [128, 8], F32)
        lg = sb("lg", [128, 8], F32)
        gc = sb("gc", [128, 8], F32)
        idb_t = sb("idb", [128, 128], BF16)
        ones_t = sb("ones", [128, 128], BF16)
        eps_t = sb("eps", [128, 1], F32)
        one_t = sb("one", [128, 1], F32)
        hc_f = sb("hc_f", [128, 3, 128], F32)
        hc_b = sb("hc_b", [128, 2, 128], BF16)
        hc_z = sb("hc_z", [128, 2], F32)
        stat = sb("stat", [128, 64], F32)
        TF_t = sb("TF", [128, 2, 512], F32)
        TB_t = sb("TB", [128, 2, 512], BF16)
        RBYTES = 64 * 1024
        R_t = sb("R", [128, RBYTES // 4], F32)
        ps_all = es.enter_context(nc.psum_tensor("psall", [128, 8, 512], F32))
        ps_t = [ps_all[:, i, :] for i in range(8)]

        def carve(off, shape, dt, pat=None, **kw):
            n = int(np.prod(shape))
            nb = n * (4 if dt == F32 else 2)
            assert off % 4 == 0 and off + nb <= RBYTES, (off, nb)
            ap = R_t[:, off // 4:(off + nb + 3) // 4]
            if dt == BF16:
                ap = ap.bitcast(BF16)
            if len(shape) == 2:
                ap = ap.rearrange("p (a b) -> p a b", a=shape[0])
            elif len(shape) == 3:
                ap = ap.rearrange("p (a b c) -> p a b c", a=shape[0], b=shape[1])
            return ap

        ident = V(ctab[:, 0:128], ["ctab_id"])
        identb = V(idb_t[:], ["idb"])
        onesb = V(ones_t[:], ["ones"])
        epsv = V(eps_t[:, 0:1], ["eps"])

        PSKEYS = {b: [("ps", b)] for b in range(8)}

        def PS(b, cols=slice(0, 512)):
            return V(ps_t[b][:, cols], PSKEYS[b])

        def PSb(b):
            return ps_t[b].bitcast(BF16)

        def TF(i):
            return V(TF_t[:, i, :], [("TF", i)])

        def TB(i):
            return V(TB_t[:, i, :], [("TB", i)])

        def xT(k, tb):
            return V(xT_t[:, k, tb * 512:(tb + 1) * 512], [("xT", k, tb)])

        def uT(k, tb):
            return V(uT_t[:, k, tb * 512:(tb + 1) * 512], [("uT", k, tb)])

        def uTc(k, c):
            return V(uT_t[:, k, c * 128:(c + 1) * 128], [("uT", k, c // 4)])

        ring_ctr = [0]

        def ring_cols(w2d, parts):
            s = ring_ctr[0] % 4
            ring_ctr[0] += 1
            view = ring_t[:, s, :].rearrange("p (k n) -> p k n", k=8)
            src = w2d.rearrange("(k p) n -> p k n", p=128)
            keys = []
            allk = [("ring", s, i) for i in range(2)]
            for i, (d0, s0, n) in enumerate(parts):
                key = ("ring", s, i)
                wk = [key] if len(parts) == 2 else allk
                B.dma("pool", V(view[:, :, d0:d0 + n], wk), src[:, :, s0:s0 + n], key=key)
            return view, allk

        def ring_rows(w2d, r0):
            s = ring_ctr[0] % 4
            ring_ctr[0] += 1
            view = ring_t[:, s, :].rearrange("p (k n) -> p k n", k=4)
            src = w2d[r0:r0 + 512, :].rearrange("(k p) n -> p k n", p=128)
            allk = [("ring", s, i) for i in range(2)]
            B.dma("pool", V(view, allk), src, key=("ring", s, 0))
            return view, allk

        B.dma("sp", V(ctab[:, 0:128], ["ctab_id"]), ctab_d[:, 0:128], key="ctab")
        B.dma("sp", V(gains[:], ["gains"]), gains_d, key="gains")
        B.dma("sp", V(qkg[:], ["qkg"]), qkg_d, key="qkg")
        B.dma("sp", V(dec[:], ["dec"]), dec_d, key="dec")
        B.cp("dve", identb, ident)
        B.memset("dve", onesb, 1.0)
        B.memset("dve", epsv, 1e-6)
        B.memset("dve", V(one_t[:, 0:1], ["one"]), 1.0)
        B.act(V(lg[:], ["lg"]), V(dec[:], ["dec"]), AF.Exp)
        B.ts("dve", V(lg[:], ["lg"]), V(lg[:], ["lg"]), -1.0, None, ALU.mult)
        B.act(V(gc[:], ["gc"]), V(lg[:], ["lg"]), AF.Exp, scale=float(C))
        T_P1, T_M1, T_P2, T_M2, T_IP1, T_CI = [V(ctab[:, (i + 1) * 128:(i + 2) * 128], ["ctab_t"]) for i in range(6)]
        T_c1 = V(ctab[:, 896:897], ["ctab_t"])
        T_c2 = V(ctab[:, 897:898], ["ctab_t"])

        def load_x(s):
            S.barrier()
            for c in range(16):
                xi = V(R_t[:, (c % 2) * 1024:(c % 2 + 1) * 1024], [("xin", c % 2)])
                B.dma("sp", xi, x_d[s, c * 128:(c + 1) * 128, :], key=("xin", c % 2))
                for hb in range(2):
                    pb = PS(4 + (c * 2 + hb) % 4)
                    for j in range(4):
                        k = hb * 4 + j
                        B.tr(V(pb.ap[:, j * 128:(j + 1) * 128], pb.keys),
                             V(xi.ap[:, k * 128:(k + 1) * 128], xi.keys), ident)
                    dst = V(xT_t[:, hb * 4:(hb + 1) * 4, c * 128:(c + 1) * 128],
                            [("xT", hb * 4 + j, c // 4) for j in range(4)])
                    src = V(pb.ap.rearrange("p (a b) -> p a b", a=4), pb.keys)
                    B.cp("act" if hb == 0 else "dve", dst, src)

        def norm(gi):
            for tb in range(4):
                pb = PS(tb % 2)
                for k in range(8):
                    sq = TB(k % 2)
                    B.act(sq, xT(k, tb), AF.Square)
                    B.mm(pb, onesb, sq, start=(k == 0), stop=(k == 7))
                rs = TF(tb % 2)
                B.act(rs, pb, AF.Sqrt, scale=1.0 / D, bias=epsv)
                B.recip(rs, rs)
                for k in range(8):
                    B.stt("dve", uT(k, tb), xT(k, tb), V(gains[:, gi * 8 + k:gi * 8 + k + 1], ["gains"]), rs,
                          ALU.mult, ALU.mult)

        def ret_pre():
            return (ring_cols(wr_in, [(0, 0, 256), (256, 1024, 256)]), ring_cols(wr_in, [(0, 2048, 512)]))

        def retention(pre=None):
            S.barrier()
            B.dma("sp", V(ctab[:, 128:NCT], ["ctab_t"]), ctab_d[:, 128:NCT], key="ctab_t")
            o = 0
            Y_ap = carve(o, [16, 512], BF16); o += 16384
            Ytab = R_t[:, 0:4096]
            qT_ap = carve(o, [2, SEQ], BF16); o += 8192
            kT_ap = carve(o, [2, SEQ], BF16); o += 8192
            v_ap = carve(o, [16, 512], BF16); o += 16384
            TF23 = carve(o, [2, 512], F32); o += 4096
            qx_ap = carve(o, [2, 2, 128], BF16); o += 1024
            kz_ap = carve(o, [2, 256], BF16); o += 1024
            adt_ap = carve(o, [2, 128], BF16); o += 512
            Sst_ap = carve(o, [2, 2, 512], BF16); o += 4096
            ygT_ap = carve(o, [4, 512], BF16); o += 4096

            def Ytabv(which, tb):
                base = which * 2048 + tb * 512
                kb = which * 8 + tb * 2
                return V(Ytab[:, base:base + 512], [("Y", kb), ("Y", kb + 1)])

            def yb(c):
                return V(Y_ap[:, c, :], [("Y", c)])

            def qTv(m, tb):
                return V(qT_ap[:, m, tb * 512:(tb + 1) * 512], [("qT", tb)])

            def kTv(m, tb):
                return V(kT_ap[:, m, tb * 512:(tb + 1) * 512], [("kT", tb)])

            def vv(c):
                return V(v_ap[:, c, :], [("v", c)])

            TFx = [TF(0), TF(1), V(TF23[:, 0, :], [("TF", 2)]), V(TF23[:, 1, :], [("TF", 3)])]
            bankctr = [0]

            for h in range(4):
                lgf = V(lg[:, h:h + 1], ["lg"])
                lgb = V(lg[:, 4 + h:5 + h], ["lg"])
                DT = V(hc_f[:, 0, :], ["hcDT"])
                xiF = V(hc_f[:, 1, :], ["hcxiF"])
                xiB = V(hc_f[:, 2, :], ["hcxiB"])
                gIf = V(hc_b[:, 0, :], ["hcgIf"])
                gIb = V(hc_b[:, 1, :], ["hcgIb"])
                zf = V(hc_z[:, 0:1], ["hczf"])
                zb = V(hc_z[:, 1:2], ["hczb"])
                t0, t1 = TFx[2], TFx[3]
                t0s = V(t0.ap[:, 0:128], t0.keys)
                t1s = V(t1.ap[:, 0:128], t1.keys)
                B.act(t0s, T_P1, AF.Exp, scale=lgf)
                B.act(t1s, T_P2, AF.Exp, scale=lgb)
                B.tt("dve", t0s, t0s, T_M1, ALU.mult)
                B.tt("dve", t1s, t1s, T_M2, ALU.mult)
                B.tt("dve", t0s, t0s, t1s, ALU.add)
                B.ts("dve", DT, t0s, 1.0 / 16.0, None, ALU.mult)
                B.act(xiF, T_IP1, AF.Exp, scale=lgf)
                B.act(xiB, T_CI, AF.Exp, scale=lgb)
                B.act(zf, T_c1, AF.Exp, scale=lgf)
                B.act(zb, T_c2, AF.Exp, scale=lgb)
                B.ts("dve", zf, zf, 1.0 / 16.0, None, ALU.mult)
                B.ts("dve", zb, zb, 1.0 / 16.0, None, ALU.mult)
                gcf = V(gc[:, h:h + 1], ["gc"])
                gcb = V(gc[:, 4 + h:5 + h], ["gc"])

                if pre is not None:
                    (wqk, kqk), (wv, kwv) = pre
                    pre = None
                else:
                    wqk, kqk = ring_cols(wr_in, [(0, h * 256, 256), (256, 1024 + h * 256, 256)])
                    wv, kwv = ring_cols(wr_in, [(0, 2048 + h * 512, 512)])
                B.dma("sp", V(Ytab[:, 0:2048], [("Y", i) for i in range(8)]), cosr_d, key="tabc")
                B.dma("sp", V(Ytab[:, 2048:4096], [("Y", i) for i in range(8, 16)]), sinr_d, key="tabs")

                for which, dstv in ((0, qTv), (1, kTv)):
                    for tp in range(2):
                        base = (bankctr[0] % 2) * 4
                        bankctr[0] += 1
                        for m in range(2):
                            for k in range(8):
                                for j in range(2):
                                    B.mm(PS(base + m * 2 + j),
                                         V(wqk[:, k, which * 256 + m * 128: which * 256 + (m + 1) * 128], kqk),
                                         uT(k, tp * 2 + j), start=(k == 0), stop=(k == 7))
                        for j in range(2):
                            tb = tp * 2 + j
                            q1, q2 = PS(base + j), PS(base + 2 + j)
                            cs, sn = Ytabv(0, tb), Ytabv(1, tb)
                            B.tt("dve", TFx[0], q1, cs, ALU.mult)
                            B.tt("dve", TFx[1], q2, sn, ALU.mult)
                            B.tt("pool", dstv(0, tb), TFx[0], TFx[1], ALU.subtract)
                            B.tt("dve", TFx[2], q1, sn, ALU.mult)
                            B.tt("dve", TFx[3], q2, cs, ALU.mult)
                            B.tt("pool", dstv(1, tb), TFx[2], TFx[3], ALU.add)
                for c in range(16):
                    pb = PS(c % 4)
                    for k in range(8):
                        B.mm(pb, uTc(k, c), V(wv[:, k, :], kwv), start=(k == 0), stop=(k == 7))
                    B.cp("act", vv(c), pb)
                wg, kwg = ring_cols(wr_in, [(0, 4096 + h * 512, 512)])
                wo, kwo = ring_rows(wr_out, h * 512)
                if h < 3:
                    pre = (ring_cols(wr_in, [(0, (h + 1) * 256, 256), (256, 1024 + (h + 1) * 256, 256)]),
                           ring_cols(wr_in, [(0, 2048 + (h + 1) * 512, 512)]))

                def kz_make(c, zcol, buf):
                    pbk = PSb(7)
                    pv = V(pbk[:, 0:256], PSKEYS[7])
                    for m in range(2):
                        B.tr(V(pbk[:, m * 128:(m + 1) * 128], PSKEYS[7]),
                             V(kT_ap[:, m, c * 128:(c + 1) * 128], [("kT", c // 4)]), identb)
                    kz = V(kz_ap[:, buf, :], [("kz", buf)])
                    B.ts("dve", kz, pv, zcol, None, ALU.mult)
                    return kz

                def state_update(c, kz, gcol, cur, nxt, first):
                    for m in range(2):
                        pb = PS(5 + m)
                        B.mm(pb, V(kz.ap[:, m * 128:(m + 1) * 128], kz.keys), vv(c), start=True, stop=True)
                        dst = V(Sst_ap[:, nxt, m, :], [("S", nxt, m)])
                        if first:
                            B.cp("act", dst, pb)
                        else:
                            B.stt("dve", dst, V(Sst_ap[:, cur, m, :], [("S", cur, m)]), gcol, pb, ALU.mult, ALU.add)

                def Sv(cur, m):
                    return V(Sst_ap[:, cur, m, :], [("S", cur, m)])

                def qx_make(c, xi, buf, eng="pool"):
                    qx = V(qx_ap[:, buf, :, :], [("qx", buf)])
                    src = V(qT_ap[:, :, c * 128:(c + 1) * 128], [("qT", c // 4)])
                    xib = V(xi.ap.unsqueeze(1).to_broadcast([128, 2, 128]), xi.keys)
                    B.tt(eng, qx, src, xib, ALU.mult)
                    return qx

                cur = 0
                kzs = {15: kz_make(15, zb, 1)}
                for c in range(15, -1, -1):
                    if c - 1 >= 1:
                        kzs[c - 1] = kz_make(c - 1, zb, (c - 1) % 2)
                    if c < 15:
                        qx = qx_make(c, xiB, c % 2)
                        pb = PS(4)
                        for m in range(2):
                            B.mm(pb, V(qx.ap[:, m, :], qx.keys), Sv(cur, m), start=(m == 0), stop=(m == 1))
                        B.cp("act", yb(c), pb)
                    if c > 0:
                        state_update(c, kzs[c], gcb, cur, 1 - cur, first=(c == 15))
                        cur = 1 - cur

                onev = V(one_t[:, 0:1], ["one"])
                fstate = {"cur": 0}
                pbt = PSb(7)

                def stage_A(c):
                    tb = c // 4
                    cur = fstate["cur"]
                    if c < 15:
                        kz = kz_make(c, zf, c % 2)
                    pa = V(ps_t[4][:, 0:128], PSKEYS[4])
                    for m in range(2):
                        B.mm(pa, V(kT_ap[:, m, c * 128:(c + 1) * 128], [("kT", tb)]),
                             V(qT_ap[:, m, c * 128:(c + 1) * 128], [("qT", tb)]), start=(m == 0), stop=(m == 1))
                    adt = V(adt_ap[:, c % 2, :], [("adt", c % 2)])
                    B.tt("dve", adt, pa, DT, ALU.mult)
                    if c > 0:
                        qx = qx_make(c, xiF, c % 2, "dve")
                    if c < 15:
                        nxt = 1 - cur
                        state_update(c, kz, gcf, cur, nxt, first=(c == 0))
                        fstate["cur"] = nxt
                    py = PS(c % 2)
                    mlist = [(adt, vv(c))]
                    if c > 0:
                        for m in range(2):
                            mlist.append((V(qx.ap[:, m, :], qx.keys), Sv(cur, m)))
                    if c < 15:
                        mlist.append((identb, yb(c)))
                    for mi, (l_, r_) in enumerate(mlist):
                        B.mm(py, l_, r_, start=(mi == 0), stop=(mi == len(mlist) - 1))
                    pg = PS(2 + c % 2)
                    for k in range(8):
                        B.mm(pg, uTc(k, c), V(wg[:, k, :], kwg), start=(k == 0), stop=(k == 7))

                SG = [TFx[1], TFx[3]]

                def gate(c):
                    sgc = SG[c % 2]
                    B.act(sgc, PS(2 + c % 2), AF.Exp, scale=-1.0)
                    B.act(sgc, sgc, AF.Ln, bias=onev)
                    B.act(sgc, sgc, AF.Exp, scale=-1.0)

                def stage_B(c):
                    tb = c // 4
                    py = PS(c % 2)
                    pg = PS(2 + c % 2)
                    st6 = V(stat[:, 0:6], ["st6"])
                    mv = V(stat[:, 8:10], ["mv"])
                    rs = V(stat[:, 10:11], ["rs"])
                    S.op("dve", lambda e, a=st6.ap, b=py.ap: e.bn_stats(a, b), reads=_k(py), writes=_k(st6))
                    S.op("dve", lambda e, a=mv.ap, b=st6.ap: e.bn_aggr(a, b), reads=_k(st6), writes=_k(mv))
                    B.act(rs, V(stat[:, 9:10], ["mv"]), AF.Ln, bias=epsv)
                    B.act(rs, rs, AF.Exp, scale=-0.5)
                    sg = SG[c % 2]
                    if c + 1 < 16:
                        gate(c + 1)
                    yn = TFx[0]
                    B.ts("dve", yn, py, V(stat[:, 8:9], ["mv"]), rs, ALU.subtract, ALU.mult)
                    B.tt("dve", sg, pg, sg, ALU.mult)
                    ygn = TB(c % 2)
                    B.tt("dve", ygn, yn, sg, ALU.mult)
                    for j in range(4):
                        B.tr(V(pbt[:, 256 + j * 128: 256 + (j + 1) * 128], PSKEYS[7]),
                             V(ygn.ap[:, j * 128:(j + 1) * 128], ygn.keys), identb)
                    B.cp("act", V(ygT_ap[:, :, (c % 4) * 128:(c % 4 + 1) * 128], [("ygT",)]),
                         V(pbt[:, 256:768].rearrange("p (a b) -> p a b", a=4), PSKEYS[7]))
                    if c % 4 == 3:
                        obanks = [4, c % 2, 2 + c % 2]
                        for m in range(8):
                            pb = PS(obanks[m % 3])
                            for j in range(4):
                                B.mm(pb, V(wo[:, j, m * 128:(m + 1) * 128], kwo),
                                     V(ygT_ap[:, j, :], [("ygT",)]), start=(j == 0), stop=(j == 3))
                            B.tt("dve", xT(m, tb), xT(m, tb), pb, ALU.add)

                for i in range(17):
                    if i < 16:
                        stage_A(i)
                    if i == 0:
                        gate(0)
                    if i >= 1:
                        stage_B(i - 1)

        def mlp_pre(li):
            return ([ring_cols(w1_d[li], [(0, i * 512, 512)]) for i in range(2)],
                    [ring_rows(w2_d[li], i * 512) for i in range(2)])

        def mlp(li, pre=None):
            S.barrier()
            h1_ap = [carve(i * 32768, [8, SEQ], BF16) for i in range(2)]
            w1 = w1_d[li]
            w2 = w2_d[li]
            gctr = 0
            for q in range(4):
                hb = q % 2
                if q == 0 and pre is not None:
                    ws, ws2 = pre
                else:
                    ws = [ring_cols(w1, [(0, q * 1024 + i * 512, 512)]) for i in range(2)]
                    ws2 = [ring_rows(w2, q * 1024 + i * 512) for i in range(2)]
                for m in range(8):
                    wv_, kw_ = ws[m // 4]
                    col = (m % 4) * 128
                    base = (gctr % 2) * 4
                    gctr += 1
                    for k in range(8):
                        for t in range(4):
                            B.mm(PS(base + t), V(wv_[:, k, col:col + 128], kw_), uT(k, t),
                                 start=(k == 0), stop=(k == 7))
                    for t in range(4):
                        r = TF(t % 2)
                        B.act(r, PS(base + t), AF.Relu)
                        B.tt("pool" if t % 2 == 0 else "dve",
                             V(h1_ap[hb][:, m, t * 512:(t + 1) * 512], [("h1", hb, m, t)]), r, r, ALU.mult)
                for m in range(8):
                    base = (gctr % 2) * 4
                    gctr += 1
                    for kk in range(8):
                        wv_, kw_ = ws2[kk // 4]
                        for t in range(4):
                            B.mm(PS(base + t), V(wv_[:, kk % 4, m * 128:(m + 1) * 128], kw_),
                                 V(h1_ap[hb][:, kk, t * 512:(t + 1) * 512], [("h1", hb, kk, t)]),
                                 start=(kk == 0), stop=(kk == 7))
                    for t in range(4):
                        B.tt("dve", xT(m, t), xT(m, t), PS(base + t), ALU.add)

        def attn_pre():
            return [ring_cols(wa_in, [(0, i * 512, 512)]) for i in range(3)]

        def attention(pre=None):
            S.barrier()
            o = 0
            cos_ap = carve(o, [16, 64], F32); o += 4096
            sin_ap = carve(o, [16, 64], F32); o += 4096
            qT_ap = carve(o, [8, SEQ], BF16); o += 32768
            kT_ap = carve(o, [2, SEQ], BF16); o += 8192
            v_ap = carve(o, [16, 256], BF16); o += 8192
            G = o
            qn_ap = carve(G, [10, 128], F32)
            r1_ap = carve(G + 5120, [10, 64], F32)
            r2_ap = TF_t[:].rearrange("p a b -> p (a b)")[:, 0:640].rearrange("p (h d) -> p h d", d=64)
            qr_ap = ctab[:, 128:768].bitcast(BF16).rearrange("p (h d) -> p h d", d=128)
            PT_ap = carve(G, [3, 512], BF16)
            B.dma("sp", V(cos_ap, ["cosa"]), cosa_d.rearrange("(c p) f -> p c f", p=128), key="cosa")
            B.dma("sp", V(sin_ap, ["sina"]), sina_d.rearrange("(c p) f -> p c f", p=128), key="sina")
            ws = pre if pre is not None else [ring_cols(wa_in, [(0, i * 512, 512)]) for i in range(3)]
            qg = V(qkg[:, 0:128].unsqueeze(1).to_broadcast([128, 8, 128]), ["qkg"])
            kg = V(qkg[:, 128:256].unsqueeze(1).to_broadcast([128, 2, 128]), ["qkg"])
            QN, R1, R2, QR = [("qn",)], [("r1",)], [("TF", 0), ("TF", 1)], [("qr",)]

            def inproj_mm(c):
                base = (c % 2) * 3
                for k in range(8):
                    for n in range(3):
                        B.mm(PS(base + n), uTc(k, c), V(ws[n][0][:, k, :], ws[n][1]), start=(k == 0), stop=(k == 7))

            sqq = V(TB_t[:].rearrange("p a (h d) -> p (a h) d", d=128), [("sqq",)])
            sqk = V(hc_b[:], [("sqk",)])

            def stage1(c):
                base = (c % 2) * 3
                pq = V(ps_all[:, base:base + 2, :].rearrange("p a (h d) -> p (a h) d", d=128),
                       PSKEYS[base] + PSKEYS[base + 1])
                pk = V(ps_t[base + 2][:, 0:256].rearrange("p (h d) -> p h d", d=128), PSKEYS[base + 2])
                B.cp("act", V(v_ap[:, c, :], [("v", c)]), PS(base + 2, slice(256, 512)))
                sc = 32 + (c % 2) * 16
                ssq = V(stat[:, sc:sc + 8], [("ssc", c % 2, hh) for hh in range(8)])
                ssk = V(stat[:, sc + 8:sc + 10], [("ssc", c % 2, hh) for hh in range(8, 10)])
                B.act(sqq, pq, AF.Square)
                B.act(sqk, pk, AF.Square)
                S.op("dve", lambda e, o_=ssq.ap, i_=sqq.ap: e.reduce_sum(o_, i_, axis=AX.X), reads=_k(sqq), writes=_k(ssq))
                S.op("dve", lambda e, o_=ssk.ap, i_=sqk.ap: e.reduce_sum(o_, i_, axis=AX.X), reads=_k(sqk), writes=_k(ssk))
                ss = V(stat[:, sc:sc + 10], list(ssq.keys) + list(ssk.keys))
                B.act(ss, ss, AF.Sqrt, scale=1.0 / 128, bias=epsv)

            def stage2a(c):
                base = (c % 2) * 3
                sc = 32 + (c % 2) * 16
                pkeys = PSKEYS[base] + PSKEYS[base + 1] + PSKEYS[base + 2]
                pqk = V(ps_all[:, base:base + 3, :].rearrange("p a b -> p (a b)")[:, 0:1280].rearrange(
                    "p (h d) -> p h d", d=128), pkeys)
                sskeys = [("ssc", c % 2, hh) for hh in range(10)]
                ssv = V(stat[:, sc:sc + 10], sskeys)
                B.recip(ssv, ssv)
                qn = V(qn_ap, QN)
                rq = V(stat[:, sc:sc + 10].unsqueeze(2).to_broadcast([128, 10, 128]), sskeys)
                B.tt("dve", qn, pqk, rq, ALU.mult)

            def stage2b(c):
                B.tt("dve", V(qn_ap[:, 0:8, :], QN), V(qn_ap[:, 0:8, :], QN), qg, ALU.mult)
                B.tt("pool", V(qn_ap[:, 8:10, :], QN), V(qn_ap[:, 8:10, :], QN), kg, ALU.mult)
                cosb = V(cos_ap[:, c, :].unsqueeze(1).to_broadcast([128, 10, 64]), ["cosa"])
                sinb = V(sin_ap[:, c, :].unsqueeze(1).to_broadcast([128, 10, 64]), ["sina"])
                x1 = V(qn_ap[:, :, 0:64], QN)
                x2 = V(qn_ap[:, :, 64:128], QN)
                r1 = V(r1_ap, R1)
                r2 = V(r2_ap, R2)
                B.tt("dve", r1, x1, cosb, ALU.mult)
                B.tt("dve", r2, x2, sinb, ALU.mult)
                B.tt("dve", V(qr_ap[:, :, 0:64], QR), r1, r2, ALU.subtract)
                B.tt("dve", r1, x1, sinb, ALU.mult)
                B.tt("dve", r2, x2, cosb, ALU.mult)
                B.tt("dve", V(qr_ap[:, :, 64:128], QR), r1, r2, ALU.add)
                pbq = PSb(6)
                for hh in range(8):
                    B.tr(V(pbq[:, hh * 128:(hh + 1) * 128], PSKEYS[6]), V(qr_ap[:, hh, :], QR), identb)
                pbk = PSb(7)
                for hh in range(2):
                    B.tr(V(pbk[:, hh * 128:(hh + 1) * 128], PSKEYS[7]), V(qr_ap[:, 8 + hh, :], QR), identb)
                B.cp("act", V(qT_ap[:, 0:8, c * 128:(c + 1) * 128], [("qTa", i, c // 4) for i in range(8)]),
                     V(pbq[:, 0:1024].rearrange("p (a b) -> p a b", a=8), PSKEYS[6]))
                B.cp("act", V(kT_ap[:, 0:2, c * 128:(c + 1) * 128], [("kTa", i) for i in range(2)]),
                     V(pbk[:, 0:256].rearrange("p (a b) -> p a b", a=2), PSKEYS[7]))

            inproj_mm(0)
            inproj_mm(1)
            stage1(0)
            for c in range(16):
                if c + 1 < 16:
                    stage1(c + 1)
                stage2a(c)
                if c + 2 < 16:
                    inproj_mm(c + 2)
                stage2b(c)
            S.barrier()
            wo = [ring_rows(wa_out, i * 512) for i in range(2)]
            scale = 128.0 ** -0.5
            it = 0
            for kv in range(2):
                for g in range(4):
                    h = kv * 4 + g
                    for qb in range(4):
                        po = PS(3 + it % 2)
                        pd = PS(5 + it % 2)
                        it += 1
                        qv = V(qT_ap[:, h, qb * 512:(qb + 1) * 512], [("qTa", h, qb)])
                        for i in range(16 + 2):
                            if i < 16:
                                pS = PS(i % 3)
                                B.mm(pS, V(kT_ap[:, kv, i * 128:(i + 1) * 128], [("kTa", kv)]), qv, start=True, stop=True)
                                B.act(V(PT_ap[:, i % 3, :], [("PT", i % 3)]), pS, AF.Exp, scale=scale)
                            if i >= 2:
                                kt = i - 2
                                pt = V(PT_ap[:, kt % 3, :], [("PT", kt % 3)])
                                B.mm(po, V(v_ap[:, kt, kv * 128:(kv + 1) * 128], [("v", kt)]), pt,
                                     start=(kt == 0), stop=(kt == 15))
                                B.mm(pd, onesb, pt, start=(kt == 0), stop=(kt == 15))
                        rd = TF(it % 2)
                        B.recip(rd, pd)
                        B.tt("dve", uT(h, qb), po, rd, ALU.mult)
            gctr = 0
            for m in range(8):
                base = (gctr % 2) * 4
                gctr += 1
                for k in range(8):
                    for t in range(4):
                        B.mm(PS(base + t),
                             V(wo[k // 4][0][:, k % 4, m * 128:(m + 1) * 128], wo[k // 4][1]), uT(k, t),
                             start=(k == 0), stop=(k == 7))
                for t in range(4):
                    B.tt("dve", xT(m, t), xT(m, t), PS(base + t), ALU.add)

        def final(s):
            S.barrier()
            gf = V(R_t[:, 0:1024], ["gfin"])
            B.dma("sp", gf, gfin_d, key="gfin")
            ob = [V(R_t[:, 1024 * (i + 1):1024 * (i + 2)], [("ob", i)]) for i in range(2)]
            junk = V(R_t[:, 3072:3584], ["junk"])
            for c in range(16):
                obv = ob[c % 2]
                pbs = [PS((c % 2) * 2 + hb) for hb in range(2)]
                for hb in range(2):
                    for j in range(4):
                        k = hb * 4 + j
                        B.tr(V(pbs[hb].ap[:, j * 128:(j + 1) * 128], pbs[hb].keys),
                             V(xT_t[:, k, c * 128:(c + 1) * 128], [("xT", k, c // 4)]), ident)
                ssv = V(stat[:, 0:2], ["fss"])
                B.memset("dve", ssv, 0.0)
                for hb in range(2):
                    B.act(junk, pbs[hb], AF.Square, accum=V(stat[:, hb:hb + 1], ["fss"]))
                rs = V(stat[:, 2:3], ["frs"])
                B.tt("dve", rs, V(stat[:, 0:1], ["fss"]), V(stat[:, 1:2], ["fss"]), ALU.add)
                B.act(rs, rs, AF.Sqrt, scale=1.0 / D, bias=epsv)
                B.recip(rs, rs)
                for hb in range(2):
                    B.stt("dve", V(obv.ap[:, hb * 512:(hb + 1) * 512], obv.keys), pbs[hb], rs,
                          V(gf.ap[:, hb * 512:(hb + 1) * 512], gf.keys), ALU.mult, ALU.mult)
                B.dma("sp", y_d[s, c * 128:(c + 1) * 128, :], obv, key=("out", c % 2))
            for i in range(2):
                if ("out", i) not in S.final_waits:
                    S.final_waits.append(("out", i))

        for s in range(nseq):
            load_x(s)
            if "ret" in stages:
                p_ = ret_pre()
                norm(0)
                retention(p_)
            if "mlp0" in stages:
                p_ = mlp_pre(0)
                norm(1)
                mlp(0, p_)
            if "attn" in stages:
                p_ = attn_pre()
                norm(2)
                attention(p_)
            if "mlp1" in stages:
                p_ = mlp_pre(1)
                norm(3)
                mlp(1, p_)
            final(s)
        S.emit()
    return nc


def _rope_tables(seq, rot_dim):
    rows = seq // 64
    row = np.broadcast_to(np.arange(rows, dtype=np.float32)[:, None], (rows, 64)).reshape(seq)
    col = np.broadcast_to(np.arange(64, dtype=np.float32)[None, :], (rows, 64)).reshape(seq)
    per_axis = rot_dim // 2
    n_freq = per_axis // 2
    inv_freq = (np.float32(10000.0) ** (-np.arange(n_freq, dtype=np.float32) * np.float32(2.0) / np.float32(per_axis))).astype(np.float32)
    ang = np.concatenate([row[:, None] * inv_freq[None, :], col[:, None] * inv_freq[None, :]], axis=-1).astype(np.float32)
    return np.cos(ang).astype(np.float32), np.sin(ang).astype(np.float32)


def _const_table():
    j = np.arange(128, dtype=np.float32)[:, None]
    i = np.arange(128, dtype=np.float32)[None, :]
    t = np.zeros((128, NCT), np.float32)
    t[:, 0:128] = np.eye(128, dtype=np.float32)
    t[:, 128:256] = np.maximum(i - j, 0)
    t[:, 256:384] = (i >= j)
    t[:, 384:512] = np.maximum(j - i, 0)
    t[:, 512:640] = (j > i)
    t[:, 640:768] = np.broadcast_to(i + 1.0, (128, 128))
    t[:, 768:896] = np.broadcast_to(128.0 - i, (128, 128))
    t[:, 896] = 127.0 - j[:, 0]
    t[:, 897] = j[:, 0]
    return t


def make_in_maps(inputs, nseq=NSEQ, ncores=NCORES):
    f = lambda a: np.ascontiguousarray(np.asarray(a, dtype=np.float32))
    x = f(inputs["x"])
    cos_r, sin_r = _rope_tables(SEQ, 256)
    cos_a, sin_a = _rope_tables(SEQ, 128)
    nm, nl = f(inputs["norm_mix"]), f(inputs["norm_mlp"])
    gl = [nm[0], nl[0], nm[1], nl[1]]
    gains = np.concatenate([g.reshape(8, 128).T for g in gl], axis=1)
    common = {
        "ret_w_in": f(inputs["ret_w_in"][0]), "ret_w_out": f(inputs["ret_w_out"][0]),
        "mlp_w1": f(inputs["mlp_w1"]), "mlp_w2": f(inputs["mlp_w2"]),
        "attn_w_in": f(inputs["attn_w_in"][0]), "attn_w_out": f(inputs["attn_w_out"][0]),
        "ctab": _const_table(), "gains": f(gains),
        "gfinal": f(np.broadcast_to(f(inputs["final_norm"])[None, :], (128, D))),
        "qkg": f(np.broadcast_to(np.concatenate([f(inputs["attn_q_norm"][0]), f(inputs["attn_k_norm"][0])])[None, :], (128, 256))),
        "dec": f(np.broadcast_to(np.concatenate([f(inputs["ret_decay_fwd"][0]), f(inputs["ret_decay_bwd"][0])])[None, :], (128, 8))),
        "cos_r": f(cos_r.T), "sin_r": f(sin_r.T), "cos_a": f(cos_a), "sin_a": f(sin_a),
    }
    maps = []
    for c in range(ncores):
        m = dict(common)
        m["x"] = f(x[c * nseq:(c + 1) * nseq])
        maps.append(m)
    return maps


_NC_CACHE = {}


def kernel(**inputs):
    if "nc" not in _NC_CACHE:
        _NC_CACHE["nc"] = build()
    nc = _NC_CACHE["nc"]
    in_maps = make_in_maps(inputs)
    res = run_bass_kernel_spmd(nc, in_maps, core_ids=list(range(NCORES)))
    out = np.concatenate([np.asarray(r["y"]) for r in res.results], axis=0)
    return out.astype(np.float32)
```
